# Optimizing a Trainium2 kernel written in Bass

```python
import math
import jax, jax.numpy as jnp
from jax import lax
import numpy as np

D_MODEL = 1024
BATCH = 8
SEQ = 2048
DEPTH = 4

CHUNK = 64
N_MIXERS = 3
ALPHA = (2.0 * DEPTH) ** 0.25
BETA = (8.0 * DEPTH) ** -0.25
LN_EPS = 1e-5
RMS_EPS = 1e-6
SC_WIDTH = 3
MLA_HEADS = 8
QK_NOPE = 128
QK_ROPE = 64
V_HEAD = 128
Q_LORA = 3 * D_MODEL // 8
KV_LORA = D_MODEL // 4
ROPE_THETA = 10000.0
Q_BLOCK = 128
CONF_WIDTH = 31
D_FF = 4 * D_MODEL
N_A = len(range(0, DEPTH, N_MIXERS))
N_B = len(range(1, DEPTH, N_MIXERS))
N_C = len(range(2, DEPTH, N_MIXERS))

kernel_name = "hybrid_chunk_causal_deepnorm_trunk"


def _layer_norm(x, g, b):
    xf = x.astype(jnp.float32)
    mu = jnp.mean(xf, axis=-1, keepdims=True)
    var = jnp.mean(jnp.square(xf - mu), axis=-1, keepdims=True)
    y = (xf - mu) * lax.rsqrt(var + LN_EPS) * g.astype(jnp.float32) + b.astype(jnp.float32)
    return y.astype(x.dtype)


def _rms_norm(x, g):
    xf = x.astype(jnp.float32)
    y = xf * lax.rsqrt(jnp.mean(jnp.square(xf), axis=-1, keepdims=True) + RMS_EPS) * g.astype(jnp.float32)
    return y.astype(x.dtype)


def _causal_dwconv(x, w):
    k_width, c = w.shape
    return lax.conv_general_dilated(
        x, w[:, None, :].astype(x.dtype), window_strides=(1,), padding=[(k_width - 1, 0)],
        dimension_numbers=("NWC", "WIO", "NWC"), feature_group_count=c)


def _rope(x, cos, sin):
    x1, x2 = jnp.split(x, 2, axis=-1)
    c = cos[None, :, None, :].astype(x.dtype)
    s = sin[None, :, None, :].astype(x.dtype)
    return jnp.concatenate([x1 * c - x2 * s, x1 * s + x2 * c], axis=-1)


def _short_conv_mixer(x, w_in, conv_w, w_out):
    b_gate, c_gate, h = jnp.split(x @ w_in, 3, axis=-1)
    return (b_gate * _causal_dwconv(c_gate * h, conv_w)) @ w_out


def _mla_mixer(x, w_dq, g_q, w_uq, w_dkv, g_kv, w_uk, w_uv, w_o):
    bsz, seq, _ = x.shape
    pos = jnp.arange(seq, dtype=jnp.float32)
    inv_freq = ROPE_THETA ** (-jnp.arange(0, QK_ROPE, 2, dtype=jnp.float32) / QK_ROPE)
    ang = pos[:, None] * inv_freq[None, :]
    cos, sin = jnp.cos(ang), jnp.sin(ang)
    cq = _rms_norm(x @ w_dq, g_q)
    q = (cq @ w_uq).reshape(bsz, seq, MLA_HEADS, QK_NOPE + QK_ROPE)
    q_nope, q_pe = q[..., :QK_NOPE], _rope(q[..., QK_NOPE:], cos, sin)
    ckv_full = x @ w_dkv
    ckv = _rms_norm(ckv_full[..., :KV_LORA], g_kv)
    k_pe = _rope(ckv_full[..., None, KV_LORA:], cos, sin)[:, :, 0, :]
    k_nope = jnp.einsum("bsc,chd->bshd", ckv, w_uk)
    v = jnp.einsum("bsc,chd->bshd", ckv, w_uv)
    scale = (QK_NOPE + QK_ROPE) ** -0.5
    n_blk = seq // Q_BLOCK
    qn_b = q_nope.reshape(bsz, n_blk, Q_BLOCK, MLA_HEADS, QK_NOPE).transpose(1, 0, 2, 3, 4)
    qp_b = q_pe.reshape(bsz, n_blk, Q_BLOCK, MLA_HEADS, QK_ROPE).transpose(1, 0, 2, 3, 4)
    k_chunk = jnp.arange(seq) // CHUNK

    def block(args):
        qn, qp, blk = args
        s = (jnp.einsum("bqhd,bkhd->bhqk", qn, k_nope)
             + jnp.einsum("bqhd,bkd->bhqk", qp, k_pe)).astype(jnp.float32) * scale
        q_chunk = (blk * Q_BLOCK + jnp.arange(Q_BLOCK)) // CHUNK
        allowed = k_chunk[None, :] <= q_chunk[:, None]
        p = jax.nn.softmax(jnp.where(allowed[None, None], s, -jnp.inf), axis=-1).astype(v.dtype)
        return jnp.einsum("bhqk,bkhd->bqhd", p, v)

    o = lax.map(block, (qn_b, qp_b, jnp.arange(n_blk)))
    o = o.transpose(1, 0, 2, 3, 4).reshape(bsz, seq, MLA_HEADS * V_HEAD)
    return o @ w_o


def _conformer_conv_mixer(x, w_pw1, b_pw1, dw_w, dw_b, norm_g, norm_b, w_pw2, b_pw2):
    a, gate = jnp.split(x @ w_pw1 + b_pw1, 2, axis=-1)
    h = a * jax.nn.sigmoid(gate)
    h = _causal_dwconv(h, dw_w) + dw_b
    h = jax.nn.silu(_layer_norm(h, norm_g, norm_b))
    return h @ w_pw2 + b_pw2


def _sq_relu_mlp(x, w1, w2):
    return jnp.square(jax.nn.relu(x @ w1)) @ w2


def _normal(k, shape, fan_in, scale=1.0):
    return jax.random.normal(k, shape, jnp.float32) * (scale * fan_in ** -0.5)


def setup_inputs(seed: int = 0) -> dict:
    key = jax.random.key(seed)
    ks = iter(jax.random.split(key, 40))
    D = D_MODEL
    gain = lambda shape: 1.0 + 0.02 * jax.random.normal(next(ks), shape, jnp.float32)
    bias = lambda shape: 0.02 * jax.random.normal(next(ks), shape, jnp.float32)
    return {
        "x": jax.random.normal(next(ks), (BATCH, SEQ, D), jnp.float32),
        "sc_w_in": _normal(next(ks), (N_A, D, 3 * D), D),
        "sc_conv_w": _normal(next(ks), (N_A, SC_WIDTH, D), SC_WIDTH),
        "sc_w_out": _normal(next(ks), (N_A, D, D), D, BETA),
        "mla_w_dq": _normal(next(ks), (N_B, D, Q_LORA), D),
        "mla_g_q": gain((N_B, Q_LORA)),
        "mla_w_uq": _normal(next(ks), (N_B, Q_LORA, MLA_HEADS * (QK_NOPE + QK_ROPE)), Q_LORA),
        "mla_w_dkv": _normal(next(ks), (N_B, D, KV_LORA + QK_ROPE), D),
        "mla_g_kv": gain((N_B, KV_LORA)),
        "mla_w_uk": _normal(next(ks), (N_B, KV_LORA, MLA_HEADS, QK_NOPE), KV_LORA),
        "mla_w_uv": _normal(next(ks), (N_B, KV_LORA, MLA_HEADS, V_HEAD), KV_LORA, BETA),
        "mla_w_o": _normal(next(ks), (N_B, MLA_HEADS * V_HEAD, D), MLA_HEADS * V_HEAD, BETA),
        "cf_w_pw1": _normal(next(ks), (N_C, D, 2 * D), D),
        "cf_b_pw1": bias((N_C, 2 * D)),
        "cf_dw_w": _normal(next(ks), (N_C, CONF_WIDTH, D), CONF_WIDTH),
        "cf_dw_b": bias((N_C, D)),
        "cf_norm_g": gain((N_C, D)),
        "cf_norm_b": bias((N_C, D)),
        "cf_w_pw2": _normal(next(ks), (N_C, D, D), D, BETA),
        "cf_b_pw2": bias((N_C, D)),
        "ff_w1": _normal(next(ks), (DEPTH, D, D_FF), D, BETA),
        "ff_w2": _normal(next(ks), (DEPTH, D_FF, D), D_FF, BETA),
        "ln_mix_g": gain((DEPTH, D)),
        "ln_mix_b": bias((DEPTH, D)),
        "ln_ff_g": gain((DEPTH, D)),
        "ln_ff_b": bias((DEPTH, D)),
    }


def reference(x, sc_w_in, sc_conv_w, sc_w_out,
              mla_w_dq, mla_g_q, mla_w_uq, mla_w_dkv, mla_g_kv, mla_w_uk, mla_w_uv, mla_w_o,
              cf_w_pw1, cf_b_pw1, cf_dw_w, cf_dw_b, cf_norm_g, cf_norm_b, cf_w_pw2, cf_b_pw2,
              ff_w1, ff_w2, ln_mix_g, ln_mix_b, ln_ff_g, ln_ff_b):
    for i in range(DEPTH):
        m, j = i % N_MIXERS, i // N_MIXERS
        if m == 0:
            y = _short_conv_mixer(x, sc_w_in[j], sc_conv_w[j], sc_w_out[j])
        elif m == 1:
            y = _mla_mixer(x, mla_w_dq[j], mla_g_q[j], mla_w_uq[j], mla_w_dkv[j], mla_g_kv[j],
                           mla_w_uk[j], mla_w_uv[j], mla_w_o[j])
        else:
            y = _conformer_conv_mixer(x, cf_w_pw1[j], cf_b_pw1[j], cf_dw_w[j], cf_dw_b[j],
                                      cf_norm_g[j], cf_norm_b[j], cf_w_pw2[j], cf_b_pw2[j])
        x = _layer_norm(ALPHA * x + y, ln_mix_g[i], ln_mix_b[i])
        x = _layer_norm(ALPHA * x + _sq_relu_mlp(x, ff_w1[i], ff_w2[i]), ln_ff_g[i], ln_ff_b[i])
    return x
```

```python
import math
import numpy as np
import concourse.bass as bass
import concourse.mybir as mybir
from concourse.bass_utils import run_bass_kernel_spmd

F32 = mybir.dt.float32
BF16 = mybir.dt.bfloat16
AF = mybir.ActivationFunctionType
ALU = mybir.AluOpType

D = 1024
T = 2048
NCH = 8
NTT = 4
TW = 512
DEPTH = 4
ALPHA = (2.0 * DEPTH) ** 0.25
LN_EPS = 1e-5
RMS_EPS = 1e-6
HEADS = 8
QL = 384
KVL = 256
SCALE = 192.0 ** -0.5
CONF_W = 31
N_CORES = 8


class Trk:
    __slots__ = ("w", "r")

    def __init__(self):
        self.w = {}
        self.r = {}


class Tl:
    __slots__ = ("ap", "t")

    def __init__(self, ap, t=None):
        self.ap = ap
        self.t = t if t is not None else Trk()

    def __getitem__(self, key):
        return Tl(self.ap[key], self.t)


ENGS = ("pe", "act", "dve", "pool", "sp")


class Sched:
    def __init__(self):
        self.q = {e: [] for e in ENGS}
        self.g = 0
        self.lane_cnt = {}
        self.batch_lanes = set()

    def _deps(self, eng, reads, writes, is_dma):
        deps = {}

        def add(k, v):
            if deps.get(k, -1) < v:
                deps[k] = v
        for t in reads:
            for k, v in t.t.w.items():
                if (not is_dma) and k == eng and eng in ("pe", "pool"):
                    continue
                add(k, v)
        for t in writes:
            for k, v in t.t.w.items():
                if (not is_dma) and k == eng:
                    continue
                add(k, v)
            for k, v in t.t.r.items():
                if (not is_dma) and k == eng:
                    continue
                add(k, v)
        return deps

    def op(self, eng, fn, reads=(), writes=(), signal=True):
        seq = len(self.q[eng])
        deps = self._deps(eng, reads, writes, False)
        for t in reads:
            if t.t.r.get(eng, -1) < seq:
                t.t.r[eng] = seq
        for t in writes:
            t.t.w = {eng: seq}
            t.t.r = {}
        self.q[eng].append(dict(fn=fn, deps=deps, signal=signal, g=self.g, lane=None))
        self.g += 1

    def dma(self, eng, fn, lane, reads=(), writes=(), batch=False):
        deps = self._deps(eng, reads, writes, True)
        cnt = self.lane_cnt.get(lane, 0) + 16
        self.lane_cnt[lane] = cnt
        if batch:
            self.batch_lanes.add(lane)
        for t in reads:
            t.t.r[lane] = cnt
        for t in writes:
            t.t.w = {lane: cnt}
            t.t.r = {}
        self.q[eng].append(dict(fn=fn, deps=deps, signal=False, g=self.g, lane=lane))
        self.g += 1

    def finalize(self):
        self.cnt = {}
        self.gof = {}
        for e in ENGS:
            lst = self.q[e]
            for ins in reversed(lst):
                if ins["lane"] is None:
                    ins["signal"] = True
                    break
            c = 0
            cum = []
            for ins in lst:
                if ins["lane"] is None and ins["signal"]:
                    c += 1
                cum.append(c)
            n = len(lst)
            prox = [0] * n
            proxg = [0] * n
            nxt = None
            nxtg = None
            for i in range(n - 1, -1, -1):
                ins = lst[i]
                if ins["lane"] is None and ins["signal"]:
                    nxt = cum[i]
                    nxtg = ins["g"]
                prox[i] = nxt
                proxg[i] = nxtg
            self.cnt[e] = prox
            self.gof[e] = proxg
        for e in ENGS:
            waited = {}
            for ins in self.q[e]:
                waits = []
                for k, v in ins["deps"].items():
                    if k in ENGS:
                        target = self.cnt[k][v]
                        assert target is not None
                        assert self.gof[k][v] < ins["g"], "proxy signal after waiter (deadlock)"
                    else:
                        target = self.lane_cnt[k] if k in self.batch_lanes else v
                    if waited.get(k, 0) >= target:
                        continue
                    waited[k] = target
                    waits.append((k, target))
                ins["waits"] = waits


def _const_layout():
    cols = {}
    n = 0

    def add(name, ncol):
        nonlocal n
        cols[name] = n
        n += ncol
    for l in range(DEPTH):
        for nm in ("ln_mix_g", "ln_mix_b", "ln_ff_g", "ln_ff_b"):
            add((nm, l), 8)
    for j in range(2):
        for k in range(3):
            add(("sc_conv_w", j, k), 8)
    add("mla_g_q", 3)
    add("mla_g_kv", 2)
    add("cf_b_pw1", 16)
    for k in range(CONF_W):
        add(("cf_dw_w", k), 8)
    add("cf_dw_b", 8)
    add("cf_norm_g", 8)
    add("cf_norm_b", 8)
    add("cf_b_pw2", 8)
    add("stair", 1)
    return cols, n


CCOLS, NCCOL = _const_layout()

W_SHAPES = {
    "sc_w_in": [2, D, 3 * D],
    "sc_w_out": [2, D, D],
    "mla_dqkv": [D, 768],
    "mla_uq": [QL, HEADS * 256],
    "mla_w_uk": [KVL, HEADS * 128],
    "mla_w_uv": [KVL, HEADS * 128],
    "mla_w_o": [D, D],
    "cf_w_pw1": [D, 2 * D],
    "cf_w_pw2": [D, D],
    "ff_w1": [DEPTH, D, 4 * D],
    "ff_w2": [DEPTH, 4 * D, D],
}


def launch_w_shapes(layers):
    shp = {}
    nsc = len([l for l in layers if l % 3 == 0])
    for k, v in W_SHAPES.items():
        if k.startswith("sc_"):
            if nsc:
                shp[k] = [nsc] + v[1:]
        elif k.startswith("mla_"):
            if any(l % 3 == 1 for l in layers):
                shp[k] = v
        elif k.startswith("cf_"):
            if any(l % 3 == 2 for l in layers):
                shp[k] = v
        else:
            shp[k] = [len(layers)] + v[1:]
    return shp


def build_program(layers):
    nc = bass.Bass("TRN2", target_bir_lowering=False)
    dr = {}
    dr["xT"] = nc.dram_tensor("xT", [D, T], F32, kind="ExternalInput").ap()
    dr["cpk"] = nc.dram_tensor("cpk", [128, NCCOL], F32, kind="ExternalInput").ap()
    dr["rope"] = nc.dram_tensor("rope", [128, T], F32, kind="ExternalInput").ap()
    dr["ident2"] = nc.dram_tensor("ident2", [128, 128], F32, kind="ExternalInput").ap()
    for k, shp in launch_w_shapes(layers).items():
        dr[k] = nc.dram_tensor(k, shp, F32, kind="ExternalInput").ap()
    ffi = {l: i for i, l in enumerate(layers)}
    sci = {l: i for i, l in enumerate([l for l in layers if l % 3 == 0])}
    outT = nc.dram_tensor("outT", [D, T], F32, kind="ExternalOutput").ap()

    S = Sched()
    from contextlib import ExitStack
    with ExitStack() as es:
        es.enter_context(nc.allow_low_precision("bf16 matmul operands, fp32 accumulation"))

        def sb(name, shape, dt):
            return es.enter_context(nc.sbuf_tensor(name, shape, dt))

        Xb = sb("X", [128, NCH, T], F32)
        XBb = sb("XB", [128, NCH, T], BF16)
        Hb = sb("H", [128, NCH, T], BF16)
        WAb = sb("WA", [128, 8, 1024], BF16)
        WBb = sb("WB", [128, 8, 1024], BF16)
        Fb = sb("F", [128, 13, TW], F32)
        Bb = sb("B", [128, 2, TW], BF16)
        HHWb = sb("HHW", [128, 2, 544], F32)
        ROPEb = sb("ROPE", [128, T], F32)
        CPb = sb("CP", [128, NCCOL], F32)
        GSb = sb("GS", [128, 8], F32)
        EPSCb = sb("EPSC", [128, 4], F32)
        HISTb = sb("HIST", [128, 8, 32], F32)
        ONESb = sb("ONES", [128, 128], BF16)
        ID2b = sb("ID2", [128, 128], F32)
        ID2Bb = sb("ID2B", [128, 128], BF16)
        PSb = es.enter_context(nc.psum_tensor("PS", [128, 8, TW], F32))

        X = [[Tl(Xb[:, c, tt * TW:(tt + 1) * TW]) for tt in range(NTT)] for c in range(NCH)]
        XB = [[Tl(XBb[:, c, tt * TW:(tt + 1) * TW]) for tt in range(NTT)] for c in range(NCH)]
        H = [[Tl(Hb[:, c, tt * TW:(tt + 1) * TW]) for tt in range(NTT)] for c in range(NCH)]
        WA = [Tl(WAb[:, k, :]) for k in range(8)]
        WB = [Tl(WBb[:, k, :]) for k in range(8)]
        SLAB = {"A": WA, "B": WB}
        F = [Tl(Fb[:, k, :]) for k in range(13)]
        B = [Tl(Bb[:, k, :]) for k in range(2)]
        HHW = [Tl(HHWb[:, k, :]) for k in range(2)]
        ROPE = Tl(ROPEb[:, :])
        CP = Tl(CPb[:, :])
        GS = Tl(GSb[:, :])
        EPSC = Tl(EPSCb[:, :])
        HIST = [Tl(HISTb[:, k, :]) for k in range(8)]
        ONES = Tl(ONESb[:, :])
        ID2 = Tl(ID2b[:, :])
        ID2B = Tl(ID2Bb[:, :])
        PS = [Tl(PSb[:, k, :]) for k in range(8)]
        Bv = []
        for k in range(5, 13):
            v = Fb[:, k, :].bitcast(BF16)
            Bv.append(Tl(v[:, 0:TW], F[k].t))
            Bv.append(Tl(v[:, TW:2 * TW], F[k].t))

        def cv(name, c=0):
            col = CCOLS[name] + c
            return CP[:, col:col + 1]

        ps_busy = [False] * 8
        ps_next = [0]

        def ps_alloc():
            for i in range(8):
                b = (ps_next[0] + i) % 8
                if not ps_busy[b]:
                    ps_busy[b] = True
                    ps_next[0] = (b + 1) % 8
                    return b
            raise RuntimeError("no free PSUM bank")

        def ps_free(b):
            ps_busy[b] = False

        def mm(out, lhsT, rhs, start, stop, signal=None):
            S.op("pe", lambda e: e.matmul(out.ap, lhsT.ap, rhs.ap, start=start, stop=stop),
                 reads=[lhsT, rhs], writes=[out], signal=stop if signal is None else signal)

        def act(out, in_, func, bias=None, scale=None, extra_reads=()):
            kw = {}
            rd = [in_] + list(extra_reads)
            if bias is not None:
                if isinstance(bias, Tl):
                    kw["bias"] = bias.ap
                    rd.append(bias)
                else:
                    kw["bias"] = bias
            if scale is not None:
                if isinstance(scale, Tl):
                    kw["scale"] = scale.ap
                    rd.append(scale)
                else:
                    kw["scale"] = scale
            S.op("act", lambda e: e.activation(out.ap, in_.ap, func, **kw), reads=rd, writes=[out])

        def tt_op(eng, out, in0, in1, op):
            S.op(eng, lambda e: e.tensor_tensor(out.ap, in0.ap, in1.ap, op), reads=[in0, in1], writes=[out])

        def ts_op(eng, out, in0, s1, s2, op0, op1=None):
            rd = [in0]
            a1 = s1.ap if isinstance(s1, Tl) else s1
            a2 = s2.ap if isinstance(s2, Tl) else s2
            if isinstance(s1, Tl):
                rd.append(s1)
            if isinstance(s2, Tl):
                rd.append(s2)
            if op1 is None:
                S.op(eng, lambda e: e.tensor_scalar(out.ap, in0.ap, a1, None, op0), reads=rd, writes=[out])
            else:
                S.op(eng, lambda e: e.tensor_scalar(out.ap, in0.ap, a1, a2, op0, op1), reads=rd, writes=[out])

        def stt_op(eng, out, in0, sc, in1, op0, op1):
            rd = [in0, in1]
            a = sc.ap if isinstance(sc, Tl) else sc
            if isinstance(sc, Tl):
                rd.append(sc)
            S.op(eng, lambda e: e.scalar_tensor_tensor(out.ap, in0.ap, a, in1.ap, op0, op1), reads=rd, writes=[out])

        def copy_op(eng, out, in_):
            if eng == "act":
                S.op(eng, lambda e: e.copy(out.ap, in_.ap), reads=[in_], writes=[out])
            else:
                S.op(eng, lambda e: e.tensor_copy(out.ap, in_.ap), reads=[in_], writes=[out])

        def memset_op(eng, out, val):
            S.op(eng, lambda e: e.memset(out.ap, val), reads=[], writes=[out])

        def wload(dst, src_ap, lane):
            S.dma("pool", lambda e: e.dma_start(out=dst.ap, in_=src_ap), lane, reads=[], writes=[dst])

        S.dma("sp", lambda e: e.dma_start(out=CP.ap, in_=dr["cpk"]), "c", writes=[CP], batch=True)
        S.dma("sp", lambda e: e.dma_start(out=ID2.ap, in_=dr["ident2"]), "c", writes=[ID2], batch=True)
        S.dma("sp", lambda e: e.dma_start(out=ROPE.ap, in_=dr["rope"]), "c", writes=[ROPE], batch=True)
        for c in range(NCH):
            xs = [X[c][tt] for tt in range(NTT)]
            ap_out = Xb[:, c, :]
            ap_in = dr["xT"][c * 128:(c + 1) * 128, :]
            S.dma("sp", (lambda e, o=ap_out, i=ap_in: e.dma_start(out=o, in_=i)), "x", writes=xs, batch=True)
        memset_op("pool", ONES, 1.0)
        memset_op("pool", EPSC[:, 0:1], LN_EPS)
        memset_op("pool", EPSC[:, 1:2], QL * RMS_EPS)
        memset_op("pool", EPSC[:, 2:3], KVL * RMS_EPS)
        if 1 in layers:
            copy_op("dve", ID2B, ID2)
            S.op("act", lambda e: e.mul(GS.ap[:, 0:3], CP.ap[:, CCOLS["mla_g_q"]:CCOLS["mla_g_q"] + 3], math.sqrt(QL)),
                 reads=[CP], writes=[GS])
            S.op("act", lambda e: e.mul(GS.ap[:, 3:5], CP.ap[:, CCOLS["mla_g_kv"]:CCOLS["mla_g_kv"] + 2], math.sqrt(KVL)),
                 reads=[CP], writes=[GS])
        for c in range(NCH):
            for tt in range(NTT):
                copy_op("act" if (c + tt) % 2 == 0 else "dve", XB[c][tt], X[c][tt])

        MEAN, M2V, RSTD = F[0], F[1], F[2]

        def ln_stats(srcs, zb_dst, n, sq_only=False):
            b1 = ps_alloc()
            b2 = ps_alloc()
            nn = len(srcs)
            for i, s in enumerate(srcs):
                act(zb_dst[i], s, AF.Identity)
                sq = B[i % 2]
                act(sq, s, AF.Square)
                mm(PS[b1], ONES, zb_dst[i], i == 0, i == nn - 1, signal=True)
                mm(PS[b2], ONES, sq, i == 0, i == nn - 1, signal=True)
            return b1, b2

        def ln_finish_stats(b1, b2, n, eps):
            act(MEAN, PS[b1], AF.Identity, scale=1.0 / n)
            act(M2V, PS[b1], AF.Square, scale=1.0 / n)
            stt_op("dve", M2V, PS[b2], 1.0 / n, M2V, ALU.mult, ALU.subtract)
            act(RSTD, M2V, AF.Sqrt, bias=EPSC[:, 0:1])
            S.op("dve", lambda e: e.reciprocal(RSTD.ap, RSTD.ap), reads=[RSTD], writes=[RSTD])
            ps_free(b1)
            ps_free(b2)

        def layernorm(tt, gname, bname):
            srcs = [X[c][tt] for c in range(NCH)]
            b1, b2 = ln_stats(srcs, [XB[c][tt] for c in range(NCH)], D)
            ln_finish_stats(b1, b2, D, LN_EPS)
            for c in range(NCH):
                tt_op("pool", X[c][tt], X[c][tt], MEAN, ALU.subtract)
                tt_op("dve", X[c][tt], X[c][tt], RSTD, ALU.mult)
                act(X[c][tt], X[c][tt], AF.Identity, bias=cv(bname, c), scale=cv(gname, c))
                copy_op("pool", XB[c][tt], X[c][tt])

        def resid_evac(i, tt, b, first=True, bias=None):
            if first:
                stt_op("dve", X[i][tt], X[i][tt], ALPHA, PS[b], ALU.mult, ALU.add)
            else:
                tt_op("dve", X[i][tt], X[i][tt], PS[b], ALU.add)
            if bias is not None:
                ts_op("dve", X[i][tt], X[i][tt], bias, None, ALU.add)
            ps_free(b)

        def out_proj(slab, src, gname, bname, bias_name=None):
            for tt in range(NTT):
                for i in range(NCH):
                    b = ps_alloc()
                    for c in range(NCH):
                        mm(PS[b], slab[c][:, i * 128:(i + 1) * 128], src[c][tt], c == 0, c == NCH - 1)
                    resid_evac(i, tt, b, True, cv(bias_name, i) if bias_name else None)
                if tt > 0:
                    layernorm(tt - 1, gname, bname)
            layernorm(NTT - 1, gname, bname)

        stages = []

        def rows(apx, c):
            return apx[c * 128:(c + 1) * 128]

        rr = [0]

        def add_ffn(l):
            for g in range(4):
                def load1(slabs, l=l, g=g):
                    sl = SLAB[slabs[0]]
                    for c in range(8):
                        wload(sl[c], dr["ff_w1"][ffi[l], c * 128:(c + 1) * 128, g * 1024:(g + 1) * 1024], slabs[0] + str(c))

                def comp1(slabs, l=l, g=g):
                    sl = SLAB[slabs[0]]
                    for j in range(8):
                        for tt in range(NTT):
                            b = ps_alloc()
                            for c in range(NCH):
                                mm(PS[b], sl[c][:, j * 128:(j + 1) * 128], XB[c][tt], c == 0, c == NCH - 1)
                            R = B[rr[0] % 2]
                            rr[0] += 1
                            act(R, PS[b], AF.Relu)
                            ps_free(b)
                            tt_op("dve", H[j][tt], R, R, ALU.mult)

                def load2(slabs, l=l, g=g):
                    sl = SLAB[slabs[0]]
                    for j in range(8):
                        wload(sl[j], dr["ff_w2"][ffi[l], (g * 8 + j) * 128:(g * 8 + j + 1) * 128, :], slabs[0] + str(j))

                def comp2(slabs, l=l, g=g):
                    sl = SLAB[slabs[0]]
                    if g < 3:
                        for i in range(NCH):
                            for tt in range(NTT):
                                b = ps_alloc()
                                for j in range(8):
                                    mm(PS[b], sl[j][:, i * 128:(i + 1) * 128], H[j][tt], j == 0, j == 7)
                                resid_evac(i, tt, b, g == 0)
                    else:
                        for tt in range(NTT):
                            for i in range(NCH):
                                b = ps_alloc()
                                for j in range(8):
                                    mm(PS[b], sl[j][:, i * 128:(i + 1) * 128], H[j][tt], j == 0, j == 7)
                                resid_evac(i, tt, b, False)
                            if tt > 0:
                                layernorm(tt - 1, ("ln_ff_g", l), ("ln_ff_b", l))
                        layernorm(NTT - 1, ("ln_ff_g", l), ("ln_ff_b", l))
                stages.append((1, load1, comp1))
                stages.append((1, load2, comp2))

        def add_sc(l, j):
            w_in = dr["sc_w_in"]
            for pr in range(4):
                def load(slabs, pr=pr):
                    sl = SLAB[slabs[0]]
                    for c in range(8):
                        src = w_in[sci[l], c * 128:(c + 1) * 128, :].rearrange("p (s n) -> p s n", s=3)[:, :, pr * 256:(pr + 1) * 256]
                        dst = Tl(sl[c].ap[:, 0:768].rearrange("p (s n) -> p s n", s=3), sl[c].t)
                        wload(dst, src, slabs[0] + str(c))

                def comp(slabs, pr=pr):
                    sl = SLAB[slabs[0]]
                    HS, V = F[3], F[4]
                    for q in range(2):
                        cc = pr * 2 + q
                        for tt in range(NTT):
                            bb, bc, bh = ps_alloc(), ps_alloc(), ps_alloc()
                            for s, b in ((2, bh), (1, bc), (0, bb)):
                                for c in range(NCH):
                                    mm(PS[b], sl[c][:, s * 256 + q * 128: s * 256 + (q + 1) * 128], XB[c][tt], c == 0, c == NCH - 1)
                            act(HS, PS[bh], AF.Identity)
                            ps_free(bh)
                            U = HHW[tt % 2]
                            Up = HHW[(tt - 1) % 2]
                            if tt == 0:
                                memset_op("pool", U[:, 0:2], 0.0)
                            else:
                                copy_op("pool", U[:, 0:2], Up[:, TW:TW + 2])
                            tt_op("dve", U[:, 2:TW + 2], HS, PS[bc], ALU.mult)
                            ps_free(bc)
                            ts_op("dve", V, U[:, 2:TW + 2], cv(("sc_conv_w", j, 2), cc), None, ALU.mult)
                            stt_op("dve", V, U[:, 1:TW + 1], cv(("sc_conv_w", j, 1), cc), V, ALU.mult, ALU.add)
                            stt_op("dve", V, U[:, 0:TW], cv(("sc_conv_w", j, 0), cc), V, ALU.mult, ALU.add)
                            tt_op("dve", H[cc][tt], V, PS[bb], ALU.mult)
                            ps_free(bb)
                stages.append((1, load, comp))

            def load_o(slabs):
                sl = SLAB[slabs[0]]
                for c in range(8):
                    wload(sl[c], dr["sc_w_out"][sci[l], c * 128:(c + 1) * 128, :], slabs[0] + str(c))

            def comp_o(slabs):
                out_proj(SLAB[slabs[0]], H, ("ln_mix_g", l), ("ln_mix_b", l))
            stages.append((1, load_o, comp_o))

        def add_cf(l):
            def load1(slabs):
                for c in range(8):
                    wload(WA[c], dr["cf_w_pw1"][c * 128:(c + 1) * 128, 0:1024], "A" + str(c))
                    wload(WB[c], dr["cf_w_pw1"][c * 128:(c + 1) * 128, 1024:2048], "B" + str(c))

            def comp1(slabs):
                SG, V2 = F[3], F[4]
                k = 0
                for tt in range(NTT):
                    for cc in range(NCH):
                        ba, bg = ps_alloc(), ps_alloc()
                        for c in range(NCH):
                            mm(PS[bg], WB[c][:, cc * 128:(cc + 1) * 128], XB[c][tt], c == 0, c == NCH - 1)
                        for c in range(NCH):
                            mm(PS[ba], WA[c][:, cc * 128:(cc + 1) * 128], XB[c][tt], c == 0, c == NCH - 1)
                        act(SG, PS[bg], AF.Sigmoid, bias=cv("cf_b_pw1", 8 + cc))
                        ps_free(bg)
                        W = HHW[k % 2]
                        k += 1
                        if tt == 0:
                            memset_op("dve", W[:, 0:30], 0.0)
                        else:
                            copy_op("dve", W[:, 0:30], HIST[cc][:, 0:30])
                        stt_op("dve", W[:, 30:30 + TW], PS[ba], cv("cf_b_pw1", cc), SG, ALU.add, ALU.mult)
                        ps_free(ba)
                        if tt < NTT - 1:
                            copy_op("dve", HIST[cc][:, 0:30], W[:, TW:TW + 30])
                        V = F[5 + cc]
                        ts_op("dve", V, W[:, 0:TW], cv(("cf_dw_w", 0), cc), cv("cf_dw_b", cc), ALU.mult, ALU.add)
                        for kk in range(1, CONF_W):
                            stt_op("dve", V, W[:, kk:kk + TW], cv(("cf_dw_w", kk), cc), V, ALU.mult, ALU.add)
                    srcs = [F[5 + cc] for cc in range(NCH)]
                    b1, b2 = ln_stats(srcs, [H[cc][tt] for cc in range(NCH)], D)
                    ln_finish_stats(b1, b2, D, LN_EPS)
                    for cc in range(NCH):
                        V = F[5 + cc]
                        tt_op("pool", V, V, MEAN, ALU.subtract)
                        tt_op("dve", V, V, RSTD, ALU.mult)
                        act(H[cc][tt], V, AF.Silu, bias=cv("cf_norm_b", cc), scale=cv("cf_norm_g", cc))
            stages.append((2, load1, comp1))

            def load2(slabs):
                sl = SLAB[slabs[0]]
                for c in range(8):
                    wload(sl[c], dr["cf_w_pw2"][c * 128:(c + 1) * 128, :], slabs[0] + str(c))

            def comp2(slabs):
                out_proj(SLAB[slabs[0]], H, ("ln_mix_g", l), ("ln_mix_b", l), bias_name="cf_b_pw2")
            stages.append((1, load2, comp2))

        def add_mla(l):
            CQ = [H[0], H[1], H[2]]
            CKV = [H[3], H[4]]
            KPE = H[5]
            QN, KN = H[6], H[7]
            QR = Bv[0:4]
            VH = Bv[4:8]
            PT = Bv[8:12]
            PTD = Bv[12:16]

            def load1(slabs):
                sl = SLAB[slabs[0]]
                for c in range(8):
                    wload(sl[c][:, 0:768], dr["mla_dqkv"][c * 128:(c + 1) * 128, :], slabs[0] + str(c))

            def rms_chunks(sl, col0, nk, n, gcol, dst, tt):
                banks = []
                for k in range(nk):
                    b = ps_alloc()
                    banks.append(b)
                    for c in range(NCH):
                        mm(PS[b], sl[c][:, col0 + k * 128: col0 + (k + 1) * 128], XB[c][tt], c == 0, c == NCH - 1)
                b2 = ps_alloc()
                for k in range(nk):
                    sq = B[k % 2]
                    act(sq, PS[banks[k]], AF.Square)
                    mm(PS[b2], ONES, sq, k == 0, k == nk - 1, signal=True)
                act(RSTD, PS[b2], AF.Sqrt, bias=EPSC[:, (1 if n == QL else 2):(2 if n == QL else 3)])
                ps_free(b2)
                S.op("dve", lambda e: e.reciprocal(RSTD.ap, RSTD.ap), reads=[RSTD], writes=[RSTD])
                for k in range(nk):
                    stt_op("dve", dst[k][tt], PS[banks[k]], GS[:, gcol + k:gcol + k + 1], RSTD, ALU.mult, ALU.mult)
                    ps_free(banks[k])

            def comp1(slabs):
                sl = SLAB[slabs[0]]
                KFB = Tl(F[3].ap.bitcast(BF16)[:, 0:TW], F[3].t)
                for tt in range(NTT):
                    rms_chunks(sl, 0, 3, QL, 0, CQ, tt)
                    rms_chunks(sl, 384, 2, KVL, 3, CKV, tt)
                    b = ps_alloc()
                    for c in range(NCH):
                        mm(PS[b], sl[c][:, 640:768], XB[c][tt], c == 0, c == NCH - 1)
                    tt_op("dve", KFB, PS[b], ROPE[:, tt * TW:(tt + 1) * TW], ALU.mult)
                    ps_free(b)
                    b = ps_alloc()
                    mm(PS[b], ID2B, KFB, True, True)
                    act(KPE[tt], PS[b], AF.Identity)
                    ps_free(b)
            stages.append((1, load1, comp1))

            for hh in range(2):
                def load2(slabs, hh=hh):
                    sl = SLAB[slabs[0]]
                    for k in range(3):
                        wload(sl[k], dr["mla_uq"][k * 128:(k + 1) * 128, hh * 1024:(hh + 1) * 1024], slabs[0] + str(k))
                    for k in range(2):
                        wload(sl[3 + k], dr["mla_w_uk"][k * 128:(k + 1) * 128, :], slabs[0] + str(3 + k))
                        wload(sl[5 + k], dr["mla_w_uv"][k * 128:(k + 1) * 128, :], slabs[0] + str(5 + k))

                def comp2(slabs, hh=hh):
                    sl = SLAB[slabs[0]]
                    RL = F[4]
                    if hh == 0:
                        for j in range(4):
                            memset_op("dve", PTD[j], 0.0)
                    for hl in range(4):
                        h = hh * 4 + hl
                        for tt in range(NTT):
                            b = ps_alloc()
                            for k in range(3):
                                mm(PS[b], sl[k][:, hl * 256: hl * 256 + 128], CQ[k][tt], k == 0, k == 2)
                            act(QN[tt], PS[b], AF.Identity)
                            ps_free(b)
                            b = ps_alloc()
                            for k in range(3):
                                mm(PS[b], sl[k][:, hl * 256 + 128: hl * 256 + 256], CQ[k][tt], k == 0, k == 2)
                            tt_op("dve", QR[tt], PS[b], ROPE[:, tt * TW:(tt + 1) * TW], ALU.mult)
                            ps_free(b)
                            b = ps_alloc()
                            for k in range(2):
                                mm(PS[b], sl[3 + k][:, h * 128:(h + 1) * 128], CKV[k][tt], k == 0, k == 1)
                            act(KN[tt], PS[b], AF.Identity)
                            ps_free(b)
                        for tb4 in range(4):
                            b = ps_alloc()
                            for q4 in range(4):
                                for k in range(2):
                                    mm(PS[b][:, q4 * 128:(q4 + 1) * 128], CKV[k][tb4][:, q4 * 128:(q4 + 1) * 128],
                                       sl[5 + k][:, h * 128:(h + 1) * 128], k == 0, k == 1,
                                       signal=(q4 == 3 and k == 1))
                            copy_op("dve", VH[tb4], PS[b])
                            ps_free(b)
                        for tt in range(NTT):
                            bo, bl = ps_alloc(), ps_alloc()
                            nkb = 4 * tt + 4
                            for kb in range(nkb):
                                bs = ps_alloc()
                                kt, ko = kb // 4, (kb % 4) * 128
                                mm(PS[bs], KN[kt][:, ko:ko + 128], QN[tt], True, False)
                                mm(PS[bs], KPE[kt][:, ko:ko + 128], QR[tt], False, True)
                                j = kb - 4 * tt
                                if j < 0:
                                    p = PT[kb % 4]
                                    act(p, PS[bs], AF.Exp, scale=SCALE)
                                else:
                                    p = PTD[j]
                                    c0 = j * 128
                                    act(p[:, c0:TW], PS[bs][:, c0:TW], AF.Exp, scale=SCALE)
                                    ts_op("dve", p[:, c0:c0 + 64], p[:, c0:c0 + 64], cv("stair"), None, ALU.mult)
                                ps_free(bs)
                                mm(PS[bo], VH[kt][:, ko:ko + 128], p, kb == 0, kb == nkb - 1)
                                mm(PS[bl], ONES, p, kb == 0, kb == nkb - 1)
                            S.op("dve", lambda e, o=RL.ap, i=PS[bl].ap: e.reciprocal(o, i), reads=[PS[bl]], writes=[RL])
                            ps_free(bl)
                            tt_op("dve", XB[h][tt], RL, PS[bo], ALU.mult)
                            ps_free(bo)
                stages.append((1, load2, comp2))

            def load_o(slabs):
                sl = SLAB[slabs[0]]
                for c in range(8):
                    wload(sl[c], dr["mla_w_o"][c * 128:(c + 1) * 128, :], slabs[0] + str(c))

            def comp_o(slabs):
                out_proj(SLAB[slabs[0]], XB, ("ln_mix_g", l), ("ln_mix_b", l))
            stages.append((1, load_o, comp_o))

        for l in layers:
            m = l % 3
            if m == 0:
                add_sc(l, l // 3)
            elif m == 1:
                add_mla(l)
            else:
                add_cf(l)
            add_ffn(l)

        free = {"A": True, "B": True}
        assign = []
        nxt = ["A"]
        for (ns, lf, cf_) in stages:
            if ns == 2:
                assign.append(("A", "B"))
            else:
                assign.append((nxt[0],))
                nxt[0] = "B" if nxt[0] == "A" else "A"
        nl = 0
        for k, (ns, lf, cf_) in enumerate(stages):
            while nl < len(stages) and all(free[s] for s in assign[nl]):
                stages[nl][1](assign[nl])
                for s in assign[nl]:
                    free[s] = False
                nl += 1
            assert nl > k
            cf_(assign[k])
            for s in assign[k]:
                free[s] = True

        for c in range(NCH):
            xs = [X[c][tt] for tt in range(NTT)]
            S.dma("sp", (lambda e, o=outT[c * 128:(c + 1) * 128, :], i=Xb[:, c, :]: e.dma_start(out=o, in_=i)),
                  "o", reads=xs, batch=True)

        S.finalize()

        sem_names = list(ENGS) + sorted(S.lane_cnt.keys())
        sems = {}
        for nme in sem_names:
            sems[nme] = es.enter_context(nc.semaphore("s_" + nme))
        block = es.enter_context(nc.Block())

        def emit(eng_name, e):
            for ins in S.q[eng_name]:
                for (k, v) in ins["waits"]:
                    e.wait_ge(sems[k], v)
                bi = ins["fn"](e)
                if ins["lane"] is not None:
                    bi.then_inc(sems[ins["lane"]], 16)
                elif ins["signal"]:
                    bi.then_inc(sems[eng_name], 1)
            if eng_name == "sp":
                e.wait_ge(sems["o"], S.lane_cnt["o"])

        @block.tensor
        def _(e):
            emit("pe", e)

        @block.scalar
        def _(e):
            emit("act", e)

        @block.vector
        def _(e):
            emit("dve", e)

        @block.gpsimd
        def _(e):
            emit("pool", e)

        @block.sync
        def _(e):
            emit("sp", e)

    return nc


def _pack_consts(inp):
    cpk = np.zeros((128, NCCOL), np.float32)

    def put(name, vec):
        v = np.asarray(vec, np.float32).reshape(-1, 128)
        c0 = CCOLS[name]
        cpk[:, c0:c0 + v.shape[0]] = v.T
    for l in range(DEPTH):
        for nm in ("ln_mix_g", "ln_mix_b", "ln_ff_g", "ln_ff_b"):
            put((nm, l), inp[nm][l])
    for j in range(2):
        for k in range(3):
            put(("sc_conv_w", j, k), inp["sc_conv_w"][j, k])
    put("mla_g_q", inp["mla_g_q"][0])
    put("mla_g_kv", inp["mla_g_kv"][0])
    put("cf_b_pw1", inp["cf_b_pw1"][0])
    for k in range(CONF_W):
        put(("cf_dw_w", k), inp["cf_dw_w"][0, k])
    put("cf_dw_b", inp["cf_dw_b"][0])
    put("cf_norm_g", inp["cf_norm_g"][0])
    put("cf_norm_b", inp["cf_norm_b"][0])
    put("cf_b_pw2", inp["cf_b_pw2"][0])
    cpk[:, CCOLS["stair"]] = (np.arange(128) < 64).astype(np.float32)
    return cpk


def _rope_table():
    pos = np.arange(T, dtype=np.float32)
    inv_freq = (np.float32(10000.0) ** (-np.arange(0, 64, 2, dtype=np.float32) / np.float32(64))).astype(np.float32)
    ang = (pos[:, None] * inv_freq[None, :]).astype(np.float32)
    cos = np.cos(ang).astype(np.float32).T
    sin = np.sin(ang).astype(np.float32).T
    return np.ascontiguousarray(np.concatenate([cos, cos, -sin, sin], axis=0))


def _host_weights(inp):
    f = lambda a: np.ascontiguousarray(np.asarray(a, np.float32))
    w = {}
    w["sc_w_in"] = f(inp["sc_w_in"])
    w["sc_w_out"] = f(inp["sc_w_out"])
    dq = np.asarray(inp["mla_w_dq"][0], np.float32)
    dkv = np.asarray(inp["mla_w_dkv"][0], np.float32)
    x1, x2 = dkv[:, 256:288], dkv[:, 288:320]
    w["mla_dqkv"] = f(np.concatenate([dq, dkv[:, :256], x1, x2, x2, x1], axis=1))
    uq = np.asarray(inp["mla_w_uq"][0], np.float32).reshape(QL, HEADS, 192)
    q1, q2 = uq[:, :, 128:160], uq[:, :, 160:192]
    w["mla_uq"] = f(np.concatenate([uq[:, :, :128], q1, q2, q2, q1], axis=2).reshape(QL, HEADS * 256))
    w["mla_w_uk"] = f(np.asarray(inp["mla_w_uk"][0], np.float32).reshape(KVL, HEADS * 128))
    w["mla_w_uv"] = f(np.asarray(inp["mla_w_uv"][0], np.float32).reshape(KVL, HEADS * 128))
    w["mla_w_o"] = f(inp["mla_w_o"][0])
    w["cf_w_pw1"] = f(inp["cf_w_pw1"][0])
    w["cf_w_pw2"] = f(inp["cf_w_pw2"][0])
    w["ff_w1"] = f(inp["ff_w1"])
    w["ff_w2"] = f(inp["ff_w2"])
    w["cpk"] = _pack_consts(inp)
    w["rope"] = _rope_table()
    p = np.arange(128)
    w["ident2"] = (p[:, None] % 64 == p[None, :] % 64).astype(np.float32)
    return w


LAUNCH_GROUPS = [[0, 1, 2, 3]]


def run_layers(xT_list, w, layers, trace=False):
    nc = build_program(layers)
    shp = launch_w_shapes(layers)
    sc_idx = [l // 3 for l in layers if l % 3 == 0]
    base = {"cpk": w["cpk"], "rope": w["rope"], "ident2": w["ident2"]}
    for k in shp:
        if k.startswith("sc_"):
            base[k] = np.ascontiguousarray(w[k][sc_idx])
        elif k.startswith("ff_"):
            base[k] = np.ascontiguousarray(w[k][list(layers)])
        else:
            base[k] = w[k]
    in_maps = []
    for b in range(N_CORES):
        m = dict(base)
        m["xT"] = xT_list[b]
        in_maps.append(m)
    res = run_bass_kernel_spmd(nc, in_maps, core_ids=list(range(N_CORES)), trace=trace)
    return [np.asarray(r["outT"]) for r in res.results], res


def kernel(**inputs):
    x = np.asarray(inputs["x"], np.float32)
    w = _host_weights(inputs)
    xT = [np.ascontiguousarray(x[b].T) for b in range(N_CORES)]
    for grp in LAUNCH_GROUPS:
        xT, _ = run_layers(xT, w, grp)
    out = np.stack([np.ascontiguousarray(o.T) for o in xT], axis=0)
    return out.astype(np.float32)
```

```python
import math
import numpy as np
import concourse.bass as bass
import concourse.mybir as mybir
from concourse.bass_utils import run_bass_kernel_spmd

F32 = mybir.dt.float32
BF16 = mybir.dt.bfloat16
AF = mybir.ActivationFunctionType
ALU = mybir.AluOpType

D = 1024
T = 2048
NCH = 8
NTT = 4
TW = 512
DEPTH = 4
ALPHA = (2.0 * DEPTH) ** 0.25
LN_EPS = 1e-5
RMS_EPS = 1e-6
HEADS = 8
QL = 384
KVL = 256
SCALE = 192.0 ** -0.5
CONF_W = 31
N_CORES = 8


class Trk:
    __slots__ = ("w", "r")

    def __init__(self):
        self.w = {}
        self.r = {}


class Tl:
    __slots__ = ("ap", "t")

    def __init__(self, ap, t=None):
        self.ap = ap
        self.t = t if t is not None else Trk()

    def __getitem__(self, key):
        return Tl(self.ap[key], self.t)


ENGS = ("pe", "act", "dve", "pool", "sp")


class Sched:
    def __init__(self):
        self.q = {e: [] for e in ENGS}
        self.g = 0
        self.lane_cnt = {}
        self.batch_lanes = set()

    def _deps(self, eng, reads, writes, is_dma):
        deps = {}

        def add(k, v):
            if deps.get(k, -1) < v:
                deps[k] = v
        for t in reads:
            for k, v in t.t.w.items():
                if (not is_dma) and k == eng and eng in ("pe", "pool"):
                    continue
                add(k, v)
        for t in writes:
            for k, v in t.t.w.items():
                if (not is_dma) and k == eng:
                    continue
                add(k, v)
            for k, v in t.t.r.items():
                if (not is_dma) and k == eng:
                    continue
                add(k, v)
        return deps

    def op(self, eng, fn, reads=(), writes=(), signal=True):
        seq = len(self.q[eng])
        deps = self._deps(eng, reads, writes, False)
        for t in reads:
            if t.t.r.get(eng, -1) < seq:
                t.t.r[eng] = seq
        for t in writes:
            t.t.w = {eng: seq}
            t.t.r = {}
        self.q[eng].append(dict(fn=fn, deps=deps, signal=signal, g=self.g, lane=None))
        self.g += 1

    def dma(self, eng, fn, lane, reads=(), writes=(), batch=False):
        deps = self._deps(eng, reads, writes, True)
        cnt = self.lane_cnt.get(lane, 0) + 16
        self.lane_cnt[lane] = cnt
        if batch:
            self.batch_lanes.add(lane)
        for t in reads:
            t.t.r[lane] = cnt
        for t in writes:
            t.t.w = {lane: cnt}
            t.t.r = {}
        self.q[eng].append(dict(fn=fn, deps=deps, signal=False, g=self.g, lane=lane))
        self.g += 1

    def finalize(self):
        self.cnt = {}
        self.gof = {}
        for e in ENGS:
            lst = self.q[e]
            for ins in reversed(lst):
                if ins["lane"] is None:
                    ins["signal"] = True
                    break
            c = 0
            cum = []
            for ins in lst:
                if ins["lane"] is None and ins["signal"]:
                    c += 1
                cum.append(c)
            n = len(lst)
            prox = [0] * n
            proxg = [0] * n
            nxt = None
            nxtg = None
            for i in range(n - 1, -1, -1):
                ins = lst[i]
                if ins["lane"] is None and ins["signal"]:
                    nxt = cum[i]
                    nxtg = ins["g"]
                prox[i] = nxt
                proxg[i] = nxtg
            self.cnt[e] = prox
            self.gof[e] = proxg
        for e in ENGS:
            waited = {}
            for ins in self.q[e]:
                waits = []
                for k, v in ins["deps"].items():
                    if k in ENGS:
                        target = self.cnt[k][v]
                        assert target is not None
                        assert self.gof[k][v] < ins["g"], "proxy signal after waiter (deadlock)"
                    else:
                        target = self.lane_cnt[k] if k in self.batch_lanes else v
                    if waited.get(k, 0) >= target:
                        continue
                    waited[k] = target
                    waits.append((k, target))
                ins["waits"] = waits


def _const_layout():
    cols = {}
    n = 0

    def add(name, ncol):
        nonlocal n
        cols[name] = n
        n += ncol
    for l in range(DEPTH):
        for nm in ("ln_mix_g", "ln_mix_b", "ln_ff_g", "ln_ff_b"):
            add((nm, l), 8)
    for j in range(2):
        for k in range(3):
            add(("sc_conv_w", j, k), 8)
    add("mla_g_q", 3)
    add("mla_g_kv", 2)
    add("cf_b_pw1", 16)
    for k in range(CONF_W):
        add(("cf_dw_w", k), 8)
    add("cf_dw_b", 8)
    add("cf_norm_g", 8)
    add("cf_norm_b", 8)
    add("cf_b_pw2", 8)
    add("stair", 1)
    return cols, n


CCOLS, NCCOL = _const_layout()

W_SHAPES = {
    "sc_w_in": [2, D, 3 * D],
    "sc_w_out": [2, D, D],
    "mla_dqkv": [D, 768],
    "mla_uq": [QL, HEADS * 256],
    "mla_w_uk": [KVL, HEADS * 128],
    "mla_w_uv": [KVL, HEADS * 128],
    "mla_w_o": [D, D],
    "cf_w_pw1": [D, 2 * D],
    "cf_w_pw2": [D, D],
    "ff_w1": [DEPTH, D, 4 * D],
    "ff_w2": [DEPTH, 4 * D, D],
}


def launch_w_shapes(layers):
    shp = {}
    nsc = len([l for l in layers if l % 3 == 0])
    for k, v in W_SHAPES.items():
        if k.startswith("sc_"):
            if nsc:
                shp[k] = [nsc] + v[1:]
        elif k.startswith("mla_"):
            if any(l % 3 == 1 for l in layers):
                shp[k] = v
        elif k.startswith("cf_"):
            if any(l % 3 == 2 for l in layers):
                shp[k] = v
        else:
            shp[k] = [len(layers)] + v[1:]
    return shp


def build_program(layers):
    nc = bass.Bass("TRN2", target_bir_lowering=False)
    dr = {}
    dr["xT"] = nc.dram_tensor("xT", [D, T], F32, kind="ExternalInput").ap()
    dr["cpk"] = nc.dram_tensor("cpk", [128, NCCOL], F32, kind="ExternalInput").ap()
    dr["rope"] = nc.dram_tensor("rope", [128, T], F32, kind="ExternalInput").ap()
    dr["ident2"] = nc.dram_tensor("ident2", [128, 128], F32, kind="ExternalInput").ap()
    for k, shp in launch_w_shapes(layers).items():
        dr[k] = nc.dram_tensor(k, shp, F32, kind="ExternalInput").ap()
    ffi = {l: i for i, l in enumerate(layers)}
    sci = {l: i for i, l in enumerate([l for l in layers if l % 3 == 0])}
    outT = nc.dram_tensor("outT", [D, T], F32, kind="ExternalOutput").ap()

    S = Sched()
    from contextlib import ExitStack
    with ExitStack() as es:
        es.enter_context(nc.allow_low_precision("bf16 matmul operands, fp32 accumulation"))

        def sb(name, shape, dt):
            return es.enter_context(nc.sbuf_tensor(name, shape, dt))

        Xb = sb("X", [128, NCH, T], F32)
        XBb = sb("XB", [128, NCH, T], BF16)
        Hb = sb("H", [128, NCH, T], BF16)
        WAb = sb("WA", [128, 8, 1024], BF16)
        WBb = sb("WB", [128, 8, 1024], BF16)
        Fb = sb("F", [128, 13, TW], F32)
        Bb = sb("B", [128, 2, TW], BF16)
        HHWb = sb("HHW", [128, 2, 544], F32)
        ROPEb = sb("ROPE", [128, T], F32)
        CPb = sb("CP", [128, NCCOL], F32)
        GSb = sb("GS", [128, 8], F32)
        EPSCb = sb("EPSC", [128, 4], F32)
        HISTb = sb("HIST", [128, 8, 32], F32)
        ONESb = sb("ONES", [128, 128], BF16)
        ID2b = sb("ID2", [128, 128], F32)
        ID2Bb = sb("ID2B", [128, 128], BF16)
        PSb = es.enter_context(nc.psum_tensor("PS", [128, 8, TW], F32))

        X = [[Tl(Xb[:, c, tt * TW:(tt + 1) * TW]) for tt in range(NTT)] for c in range(NCH)]
        XB = [[Tl(XBb[:, c, tt * TW:(tt + 1) * TW]) for tt in range(NTT)] for c in range(NCH)]
        H = [[Tl(Hb[:, c, tt * TW:(tt + 1) * TW]) for tt in range(NTT)] for c in range(NCH)]
        WA = [Tl(WAb[:, k, :]) for k in range(8)]
        WB = [Tl(WBb[:, k, :]) for k in range(8)]
        SLAB = {"A": WA, "B": WB}
        F = [Tl(Fb[:, k, :]) for k in range(13)]
        B = [Tl(Bb[:, k, :]) for k in range(2)]
        HHW = [Tl(HHWb[:, k, :]) for k in range(2)]
        ROPE = Tl(ROPEb[:, :])
        CP = Tl(CPb[:, :])
        GS = Tl(GSb[:, :])
        EPSC = Tl(EPSCb[:, :])
        HIST = [Tl(HISTb[:, k, :]) for k in range(8)]
        ONES = Tl(ONESb[:, :])
        ID2 = Tl(ID2b[:, :])
        ID2B = Tl(ID2Bb[:, :])
        PS = [Tl(PSb[:, k, :]) for k in range(8)]
        Bv = []
        for k in range(5, 13):
            v = Fb[:, k, :].bitcast(BF16)
            Bv.append(Tl(v[:, 0:TW], F[k].t))
            Bv.append(Tl(v[:, TW:2 * TW], F[k].t))

        def cv(name, c=0):
            col = CCOLS[name] + c
            return CP[:, col:col + 1]

        ps_busy = [False] * 8
        ps_next = [0]

        def ps_alloc():
            for i in range(8):
                b = (ps_next[0] + i) % 8
                if not ps_busy[b]:
                    ps_busy[b] = True
                    ps_next[0] = (b + 1) % 8
                    return b
            raise RuntimeError("no free PSUM bank")

        def ps_free(b):
            ps_busy[b] = False

        def mm(out, lhsT, rhs, start, stop, signal=None):
            S.op("pe", lambda e: e.matmul(out.ap, lhsT.ap, rhs.ap, start=start, stop=stop),
                 reads=[lhsT, rhs], writes=[out], signal=stop if signal is None else signal)

        def act(out, in_, func, bias=None, scale=None, extra_reads=()):
            kw = {}
            rd = [in_] + list(extra_reads)
            if bias is not None:
                if isinstance(bias, Tl):
                    kw["bias"] = bias.ap
                    rd.append(bias)
                else:
                    kw["bias"] = bias
            if scale is not None:
                if isinstance(scale, Tl):
                    kw["scale"] = scale.ap
                    rd.append(scale)
                else:
                    kw["scale"] = scale
            S.op("act", lambda e: e.activation(out.ap, in_.ap, func, **kw), reads=rd, writes=[out])

        def tt_op(eng, out, in0, in1, op):
            S.op(eng, lambda e: e.tensor_tensor(out.ap, in0.ap, in1.ap, op), reads=[in0, in1], writes=[out])

        def ts_op(eng, out, in0, s1, s2, op0, op1=None):
            rd = [in0]
            a1 = s1.ap if isinstance(s1, Tl) else s1
            a2 = s2.ap if isinstance(s2, Tl) else s2
            if isinstance(s1, Tl):
                rd.append(s1)
            if isinstance(s2, Tl):
                rd.append(s2)
            if op1 is None:
                S.op(eng, lambda e: e.tensor_scalar(out.ap, in0.ap, a1, None, op0), reads=rd, writes=[out])
            else:
                S.op(eng, lambda e: e.tensor_scalar(out.ap, in0.ap, a1, a2, op0, op1), reads=rd, writes=[out])

        def stt_op(eng, out, in0, sc, in1, op0, op1):
            rd = [in0, in1]
            a = sc.ap if isinstance(sc, Tl) else sc
            if isinstance(sc, Tl):
                rd.append(sc)
            S.op(eng, lambda e: e.scalar_tensor_tensor(out.ap, in0.ap, a, in1.ap, op0, op1), reads=rd, writes=[out])

        def copy_op(eng, out, in_):
            if eng == "act":
                S.op(eng, lambda e: e.copy(out.ap, in_.ap), reads=[in_], writes=[out])
            else:
                S.op(eng, lambda e: e.tensor_copy(out.ap, in_.ap), reads=[in_], writes=[out])

        def memset_op(eng, out, val):
            S.op(eng, lambda e: e.memset(out.ap, val), reads=[], writes=[out])

        def wload(dst, src_ap, lane):
            S.dma("pool", lambda e: e.dma_start(out=dst.ap, in_=src_ap), lane, reads=[], writes=[dst])

        S.dma("sp", lambda e: e.dma_start(out=CP.ap, in_=dr["cpk"]), "c", writes=[CP], batch=True)
        S.dma("sp", lambda e: e.dma_start(out=ID2.ap, in_=dr["ident2"]), "c", writes=[ID2], batch=True)
        S.dma("sp", lambda e: e.dma_start(out=ROPE.ap, in_=dr["rope"]), "c", writes=[ROPE], batch=True)
        for c in range(NCH):
            xs = [X[c][tt] for tt in range(NTT)]
            ap_out = Xb[:, c, :]
            ap_in = dr["xT"][c * 128:(c + 1) * 128, :]
            S.dma("sp", (lambda e, o=ap_out, i=ap_in: e.dma_start(out=o, in_=i)), "x", writes=xs, batch=True)
        memset_op("pool", ONES, 1.0)
        memset_op("pool", EPSC[:, 0:1], LN_EPS)
        memset_op("pool", EPSC[:, 1:2], QL * RMS_EPS)
        memset_op("pool", EPSC[:, 2:3], KVL * RMS_EPS)
        if 1 in layers:
            copy_op("dve", ID2B, ID2)
            S.op("act", lambda e: e.mul(GS.ap[:, 0:3], CP.ap[:, CCOLS["mla_g_q"]:CCOLS["mla_g_q"] + 3], math.sqrt(QL)),
                 reads=[CP], writes=[GS])
            S.op("act", lambda e: e.mul(GS.ap[:, 3:5], CP.ap[:, CCOLS["mla_g_kv"]:CCOLS["mla_g_kv"] + 2], math.sqrt(KVL)),
                 reads=[CP], writes=[GS])
        for c in range(NCH):
            for tt in range(NTT):
                copy_op("act" if (c + tt) % 2 == 0 else "dve", XB[c][tt], X[c][tt])

        MEAN, M2V, RSTD = F[0], F[1], F[2]

        def ln_stats(srcs, zb_dst, n, sq_only=False):
            b1 = ps_alloc()
            b2 = ps_alloc()
            nn = len(srcs)
            for i, s in enumerate(srcs):
                act(zb_dst[i], s, AF.Identity)
                sq = B[i % 2]
                act(sq, s, AF.Square)
                mm(PS[b1], ONES, zb_dst[i], i == 0, i == nn - 1, signal=True)
                mm(PS[b2], ONES, sq, i == 0, i == nn - 1, signal=True)
            return b1, b2

        def ln_finish_stats(b1, b2, n, eps):
            act(MEAN, PS[b1], AF.Identity, scale=1.0 / n)
            act(M2V, PS[b1], AF.Square, scale=1.0 / n)
            stt_op("dve", M2V, PS[b2], 1.0 / n, M2V, ALU.mult, ALU.subtract)
            act(RSTD, M2V, AF.Sqrt, bias=EPSC[:, 0:1])
            S.op("dve", lambda e: e.reciprocal(RSTD.ap, RSTD.ap), reads=[RSTD], writes=[RSTD])
            ps_free(b1)
            ps_free(b2)

        def layernorm(tt, gname, bname):
            srcs = [X[c][tt] for c in range(NCH)]
            b1, b2 = ln_stats(srcs, [XB[c][tt] for c in range(NCH)], D)
            ln_finish_stats(b1, b2, D, LN_EPS)
            for c in range(NCH):
                tt_op("pool", X[c][tt], X[c][tt], MEAN, ALU.subtract)
                tt_op("dve", X[c][tt], X[c][tt], RSTD, ALU.mult)
                act(XB[c][tt], X[c][tt], AF.Identity, bias=cv(bname, c), scale=cv(gname, c))
                act(X[c][tt], X[c][tt], AF.Identity, bias=cv(bname, c), scale=cv(gname, c))

        def resid_evac(i, tt, b, first=True, bias=None):
            if first:
                stt_op("dve", X[i][tt], X[i][tt], ALPHA, PS[b], ALU.mult, ALU.add)
            else:
                tt_op("dve", X[i][tt], X[i][tt], PS[b], ALU.add)
            if bias is not None:
                ts_op("dve", X[i][tt], X[i][tt], bias, None, ALU.add)
            ps_free(b)

        def out_proj(slab, src, gname, bname, bias_name=None):
            for tt in range(NTT):
                for i in range(NCH):
                    b = ps_alloc()
                    for c in range(NCH):
                        mm(PS[b], slab[c][:, i * 128:(i + 1) * 128], src[c][tt], c == 0, c == NCH - 1)
                    resid_evac(i, tt, b, True, cv(bias_name, i) if bias_name else None)
                if tt > 0:
                    layernorm(tt - 1, gname, bname)
            layernorm(NTT - 1, gname, bname)

        stages = []

        def rows(apx, c):
            return apx[c * 128:(c + 1) * 128]

        rr = [0]

        def add_ffn(l):
            for g in range(4):
                def load1(slabs, l=l, g=g):
                    sl = SLAB[slabs[0]]
                    for c in range(8):
                        wload(sl[c], dr["ff_w1"][ffi[l], c * 128:(c + 1) * 128, g * 1024:(g + 1) * 1024], slabs[0] + str(c))

                def comp1(slabs, l=l, g=g):
                    sl = SLAB[slabs[0]]
                    order = [(j, tt) for j in range(8) for tt in range(NTT)]
                    if g == 0:
                        order = [(j, tt) for tt in range(NTT) for j in range(8)]
                    for (j, tt) in order:
                        if True:
                            b = ps_alloc()
                            for c in range(NCH):
                                mm(PS[b], sl[c][:, j * 128:(j + 1) * 128], XB[c][tt], c == 0, c == NCH - 1)
                            R = B[rr[0] % 2]
                            rr[0] += 1
                            act(R, PS[b], AF.Relu)
                            ps_free(b)
                            tt_op("dve", H[j][tt], R, R, ALU.mult)

                def load2(slabs, l=l, g=g):
                    sl = SLAB[slabs[0]]
                    for j in range(8):
                        wload(sl[j], dr["ff_w2"][ffi[l], (g * 8 + j) * 128:(g * 8 + j + 1) * 128, :], slabs[0] + str(j))

                def comp2(slabs, l=l, g=g):
                    sl = SLAB[slabs[0]]
                    if g < 3:
                        for i in range(NCH):
                            for tt in range(NTT):
                                b = ps_alloc()
                                for j in range(8):
                                    mm(PS[b], sl[j][:, i * 128:(i + 1) * 128], H[j][tt], j == 0, j == 7)
                                resid_evac(i, tt, b, g == 0)
                    else:
                        for tt in range(NTT):
                            for i in range(NCH):
                                b = ps_alloc()
                                for j in range(8):
                                    mm(PS[b], sl[j][:, i * 128:(i + 1) * 128], H[j][tt], j == 0, j == 7)
                                resid_evac(i, tt, b, False)
                            if tt > 0:
                                layernorm(tt - 1, ("ln_ff_g", l), ("ln_ff_b", l))
                        layernorm(NTT - 1, ("ln_ff_g", l), ("ln_ff_b", l))
                stages.append((1, load1, comp1))
                stages.append((1, load2, comp2))

        def add_sc(l, j):
            w_in = dr["sc_w_in"]
            for pr in range(4):
                def load(slabs, pr=pr):
                    sl = SLAB[slabs[0]]
                    for c in range(8):
                        src = w_in[sci[l], c * 128:(c + 1) * 128, :].rearrange("p (s n) -> p s n", s=3)[:, :, pr * 256:(pr + 1) * 256]
                        dst = Tl(sl[c].ap[:, 0:768].rearrange("p (s n) -> p s n", s=3), sl[c].t)
                        wload(dst, src, slabs[0] + str(c))

                def comp(slabs, pr=pr):
                    sl = SLAB[slabs[0]]
                    HS, V = F[3], F[4]
                    order = [(q, tt) for q in range(2) for tt in range(NTT)]
                    for (q, tt) in order:
                        cc = pr * 2 + q
                        if True:
                            bb, bc, bh = ps_alloc(), ps_alloc(), ps_alloc()
                            for s, b in ((2, bh), (1, bc), (0, bb)):
                                for c in range(NCH):
                                    mm(PS[b], sl[c][:, s * 256 + q * 128: s * 256 + (q + 1) * 128], XB[c][tt], c == 0, c == NCH - 1)
                            act(HS, PS[bh], AF.Identity)
                            ps_free(bh)
                            U = HHW[tt % 2]
                            Up = HHW[(tt - 1) % 2]
                            if tt == 0:
                                memset_op("pool", U[:, 0:2], 0.0)
                            else:
                                copy_op("pool", U[:, 0:2], Up[:, TW:TW + 2])
                            tt_op("dve", U[:, 2:TW + 2], HS, PS[bc], ALU.mult)
                            ps_free(bc)
                            ts_op("dve", V, U[:, 2:TW + 2], cv(("sc_conv_w", j, 2), cc), None, ALU.mult)
                            stt_op("dve", V, U[:, 1:TW + 1], cv(("sc_conv_w", j, 1), cc), V, ALU.mult, ALU.add)
                            stt_op("dve", V, U[:, 0:TW], cv(("sc_conv_w", j, 0), cc), V, ALU.mult, ALU.add)
                            tt_op("dve", H[cc][tt], V, PS[bb], ALU.mult)
                            ps_free(bb)
                stages.append((1, load, comp))

            def load_o(slabs):
                sl = SLAB[slabs[0]]
                for c in range(8):
                    wload(sl[c], dr["sc_w_out"][sci[l], c * 128:(c + 1) * 128, :], slabs[0] + str(c))

            def comp_o(slabs):
                out_proj(SLAB[slabs[0]], H, ("ln_mix_g", l), ("ln_mix_b", l))
            stages.append((1, load_o, comp_o))

        def add_cf(l):
            def load1(slabs):
                for c in range(8):
                    wload(WA[c], dr["cf_w_pw1"][c * 128:(c + 1) * 128, 0:1024], "A" + str(c))
                    wload(WB[c], dr["cf_w_pw1"][c * 128:(c + 1) * 128, 1024:2048], "B" + str(c))

            def comp1(slabs):
                SG, V2 = F[3], F[4]
                k = 0
                for tt in range(NTT):
                    for cc in range(NCH):
                        ba, bg = ps_alloc(), ps_alloc()
                        for c in range(NCH):
                            mm(PS[bg], WB[c][:, cc * 128:(cc + 1) * 128], XB[c][tt], c == 0, c == NCH - 1)
                        for c in range(NCH):
                            mm(PS[ba], WA[c][:, cc * 128:(cc + 1) * 128], XB[c][tt], c == 0, c == NCH - 1)
                        act(SG, PS[bg], AF.Sigmoid, bias=cv("cf_b_pw1", 8 + cc))
                        ps_free(bg)
                        W = HHW[k % 2]
                        k += 1
                        if tt == 0:
                            memset_op("dve", W[:, 0:30], 0.0)
                        else:
                            copy_op("dve", W[:, 0:30], HIST[cc][:, 0:30])
                        stt_op("dve", W[:, 30:30 + TW], PS[ba], cv("cf_b_pw1", cc), SG, ALU.add, ALU.mult)
                        ps_free(ba)
                        if tt < NTT - 1:
                            copy_op("dve", HIST[cc][:, 0:30], W[:, TW:TW + 30])
                        V = F[5 + cc]
                        ts_op("dve", V, W[:, 0:TW], cv(("cf_dw_w", 0), cc), cv("cf_dw_b", cc), ALU.mult, ALU.add)
                        for kk in range(1, CONF_W):
                            stt_op("dve", V, W[:, kk:kk + TW], cv(("cf_dw_w", kk), cc), V, ALU.mult, ALU.add)
                    srcs = [F[5 + cc] for cc in range(NCH)]
                    b1, b2 = ln_stats(srcs, [H[cc][tt] for cc in range(NCH)], D)
                    ln_finish_stats(b1, b2, D, LN_EPS)
                    for cc in range(NCH):
                        V = F[5 + cc]
                        tt_op("pool", V, V, MEAN, ALU.subtract)
                        tt_op("dve", V, V, RSTD, ALU.mult)
                        act(H[cc][tt], V, AF.Silu, bias=cv("cf_norm_b", cc), scale=cv("cf_norm_g", cc))
            stages.append((2, load1, comp1))

            def load2(slabs):
                sl = SLAB[slabs[0]]
                for c in range(8):
                    wload(sl[c], dr["cf_w_pw2"][c * 128:(c + 1) * 128, :], slabs[0] + str(c))

            def comp2(slabs):
                out_proj(SLAB[slabs[0]], H, ("ln_mix_g", l), ("ln_mix_b", l), bias_name="cf_b_pw2")
            stages.append((1, load2, comp2))

        def add_mla(l):
            CQ = [H[0], H[1], H[2]]
            CKV = [H[3], H[4]]
            KPE = H[5]
            QN, KN = H[6], H[7]
            QR = Bv[0:4]
            VH = Bv[4:8]
            PT = Bv[8:12]
            PTD = Bv[12:16]

            def load1(slabs):
                sl = SLAB[slabs[0]]
                for c in range(8):
                    wload(sl[c][:, 0:768], dr["mla_dqkv"][c * 128:(c + 1) * 128, :], slabs[0] + str(c))

            def rms_chunks(sl, col0, nk, n, gcol, dst, tt):
                banks = []
                for k in range(nk):
                    b = ps_alloc()
                    banks.append(b)
                    for c in range(NCH):
                        mm(PS[b], sl[c][:, col0 + k * 128: col0 + (k + 1) * 128], XB[c][tt], c == 0, c == NCH - 1)
                b2 = ps_alloc()
                for k in range(nk):
                    sq = B[k % 2]
                    act(sq, PS[banks[k]], AF.Square)
                    mm(PS[b2], ONES, sq, k == 0, k == nk - 1, signal=True)
                act(RSTD, PS[b2], AF.Sqrt, bias=EPSC[:, (1 if n == QL else 2):(2 if n == QL else 3)])
                ps_free(b2)
                S.op("dve", lambda e: e.reciprocal(RSTD.ap, RSTD.ap), reads=[RSTD], writes=[RSTD])
                for k in range(nk):
                    stt_op("dve", dst[k][tt], PS[banks[k]], GS[:, gcol + k:gcol + k + 1], RSTD, ALU.mult, ALU.mult)
                    ps_free(banks[k])

            def comp1(slabs):
                sl = SLAB[slabs[0]]
                KFB = Tl(F[3].ap.bitcast(BF16)[:, 0:TW], F[3].t)
                for tt in range(NTT):
                    rms_chunks(sl, 0, 3, QL, 0, CQ, tt)
                    rms_chunks(sl, 384, 2, KVL, 3, CKV, tt)
                    b = ps_alloc()
                    for c in range(NCH):
                        mm(PS[b], sl[c][:, 640:768], XB[c][tt], c == 0, c == NCH - 1)
                    tt_op("dve", KFB, PS[b], ROPE[:, tt * TW:(tt + 1) * TW], ALU.mult)
                    ps_free(b)
                    b = ps_alloc()
                    mm(PS[b], ID2B, KFB, True, True)
                    act(KPE[tt], PS[b], AF.Identity)
                    ps_free(b)
            stages.append((1, load1, comp1))

            for hh in range(2):
                def load2(slabs, hh=hh):
                    sl = SLAB[slabs[0]]
                    for k in range(3):
                        wload(sl[k], dr["mla_uq"][k * 128:(k + 1) * 128, hh * 1024:(hh + 1) * 1024], slabs[0] + str(k))
                    for k in range(2):
                        wload(sl[3 + k], dr["mla_w_uk"][k * 128:(k + 1) * 128, :], slabs[0] + str(3 + k))
                        wload(sl[5 + k], dr["mla_w_uv"][k * 128:(k + 1) * 128, :], slabs[0] + str(5 + k))

                def comp2(slabs, hh=hh):
                    sl = SLAB[slabs[0]]
                    RL = F[4]
                    if hh == 0:
                        for j in range(4):
                            memset_op("dve", PTD[j], 0.0)
                    for hl in range(4):
                        h = hh * 4 + hl
                        for tt in range(NTT):
                            b = ps_alloc()
                            for k in range(3):
                                mm(PS[b], sl[k][:, hl * 256: hl * 256 + 128], CQ[k][tt], k == 0, k == 2)
                            act(QN[tt], PS[b], AF.Identity)
                            ps_free(b)
                            b = ps_alloc()
                            for k in range(3):
                                mm(PS[b], sl[k][:, hl * 256 + 128: hl * 256 + 256], CQ[k][tt], k == 0, k == 2)
                            tt_op("dve", QR[tt], PS[b], ROPE[:, tt * TW:(tt + 1) * TW], ALU.mult)
                            ps_free(b)
                            b = ps_alloc()
                            for k in range(2):
                                mm(PS[b], sl[3 + k][:, h * 128:(h + 1) * 128], CKV[k][tt], k == 0, k == 1)
                            act(KN[tt], PS[b], AF.Identity)
                            ps_free(b)
                        for tb4 in range(4):
                            b = ps_alloc()
                            for q4 in range(4):
                                for k in range(2):
                                    mm(PS[b][:, q4 * 128:(q4 + 1) * 128], CKV[k][tb4][:, q4 * 128:(q4 + 1) * 128],
                                       sl[5 + k][:, h * 128:(h + 1) * 128], k == 0, k == 1,
                                       signal=(q4 == 3 and k == 1))
                            copy_op("dve", VH[tb4], PS[b])
                            ps_free(b)
                        for tt in range(NTT):
                            bo, bl = ps_alloc(), ps_alloc()
                            nkb = 4 * tt + 4
                            for kb in range(nkb):
                                bs = ps_alloc()
                                kt, ko = kb // 4, (kb % 4) * 128
                                mm(PS[bs], KN[kt][:, ko:ko + 128], QN[tt], True, False)
                                mm(PS[bs], KPE[kt][:, ko:ko + 128], QR[tt], False, True)
                                j = kb - 4 * tt
                                if j < 0:
                                    p = PT[kb % 4]
                                    act(p, PS[bs], AF.Exp, scale=SCALE)
                                else:
                                    p = PTD[j]
                                    c0 = j * 128
                                    act(p[:, c0:TW], PS[bs][:, c0:TW], AF.Exp, scale=SCALE)
                                    ts_op("dve", p[:, c0:c0 + 64], p[:, c0:c0 + 64], cv("stair"), None, ALU.mult)
                                ps_free(bs)
                                mm(PS[bo], VH[kt][:, ko:ko + 128], p, kb == 0, kb == nkb - 1)
                                mm(PS[bl], ONES, p, kb == 0, kb == nkb - 1)
                            S.op("dve", lambda e, o=RL.ap, i=PS[bl].ap: e.reciprocal(o, i), reads=[PS[bl]], writes=[RL])
                            ps_free(bl)
                            tt_op("dve", XB[h][tt], RL, PS[bo], ALU.mult)
                            ps_free(bo)
                stages.append((1, load2, comp2))

            def load_o(slabs):
                sl = SLAB[slabs[0]]
                for c in range(8):
                    wload(sl[c], dr["mla_w_o"][c * 128:(c + 1) * 128, :], slabs[0] + str(c))

            def comp_o(slabs):
                out_proj(SLAB[slabs[0]], XB, ("ln_mix_g", l), ("ln_mix_b", l))
            stages.append((1, load_o, comp_o))

        for l in layers:
            m = l % 3
            if m == 0:
                add_sc(l, l // 3)
            elif m == 1:
                add_mla(l)
            else:
                add_cf(l)
            add_ffn(l)

        free = {"A": True, "B": True}
        assign = []
        nxt = ["A"]
        for (ns, lf, cf_) in stages:
            if ns == 2:
                assign.append(("A", "B"))
            else:
                assign.append((nxt[0],))
                nxt[0] = "B" if nxt[0] == "A" else "A"
        nl = 0
        for k, (ns, lf, cf_) in enumerate(stages):
            while nl < len(stages) and all(free[s] for s in assign[nl]):
                stages[nl][1](assign[nl])
                for s in assign[nl]:
                    free[s] = False
                nl += 1
            assert nl > k
            cf_(assign[k])
            for s in assign[k]:
                free[s] = True

        for c in range(NCH):
            xs = [X[c][tt] for tt in range(NTT)]
            S.dma("sp", (lambda e, o=outT[c * 128:(c + 1) * 128, :], i=Xb[:, c, :]: e.dma_start(out=o, in_=i)),
                  "o", reads=xs, batch=True)

        S.finalize()

        sem_names = list(ENGS) + sorted(S.lane_cnt.keys())
        sems = {}
        for nme in sem_names:
            sems[nme] = es.enter_context(nc.semaphore("s_" + nme))
        block = es.enter_context(nc.Block())

        def emit(eng_name, e):
            for ins in S.q[eng_name]:
                for (k, v) in ins["waits"]:
                    e.wait_ge(sems[k], v)
                bi = ins["fn"](e)
                if ins["lane"] is not None:
                    bi.then_inc(sems[ins["lane"]], 16)
                elif ins["signal"]:
                    bi.then_inc(sems[eng_name], 1)
            if eng_name == "sp":
                e.wait_ge(sems["o"], S.lane_cnt["o"])

        @block.tensor
        def _(e):
            emit("pe", e)

        @block.scalar
        def _(e):
            emit("act", e)

        @block.vector
        def _(e):
            emit("dve", e)

        @block.gpsimd
        def _(e):
            emit("pool", e)

        @block.sync
        def _(e):
            emit("sp", e)

    return nc


def _pack_consts(inp):
    cpk = np.zeros((128, NCCOL), np.float32)

    def put(name, vec):
        v = np.asarray(vec, np.float32).reshape(-1, 128)
        c0 = CCOLS[name]
        cpk[:, c0:c0 + v.shape[0]] = v.T
    for l in range(DEPTH):
        for nm in ("ln_mix_g", "ln_mix_b", "ln_ff_g", "ln_ff_b"):
            put((nm, l), inp[nm][l])
    for j in range(2):
        for k in range(3):
            put(("sc_conv_w", j, k), inp["sc_conv_w"][j, k])
    put("mla_g_q", inp["mla_g_q"][0])
    put("mla_g_kv", inp["mla_g_kv"][0])
    put("cf_b_pw1", inp["cf_b_pw1"][0])
    for k in range(CONF_W):
        put(("cf_dw_w", k), inp["cf_dw_w"][0, k])
    put("cf_dw_b", inp["cf_dw_b"][0])
    put("cf_norm_g", inp["cf_norm_g"][0])
    put("cf_norm_b", inp["cf_norm_b"][0])
    put("cf_b_pw2", inp["cf_b_pw2"][0])
    cpk[:, CCOLS["stair"]] = (np.arange(128) < 64).astype(np.float32)
    return cpk


def _rope_table():
    pos = np.arange(T, dtype=np.float32)
    inv_freq = (np.float32(10000.0) ** (-np.arange(0, 64, 2, dtype=np.float32) / np.float32(64))).astype(np.float32)
    ang = (pos[:, None] * inv_freq[None, :]).astype(np.float32)
    cos = np.cos(ang).astype(np.float32).T
    sin = np.sin(ang).astype(np.float32).T
    return np.ascontiguousarray(np.concatenate([cos, cos, -sin, sin], axis=0))


def _host_weights(inp):
    f = lambda a: np.ascontiguousarray(np.asarray(a, np.float32))
    w = {}
    w["sc_w_in"] = f(inp["sc_w_in"])
    w["sc_w_out"] = f(inp["sc_w_out"])
    dq = np.asarray(inp["mla_w_dq"][0], np.float32)
    dkv = np.asarray(inp["mla_w_dkv"][0], np.float32)
    x1, x2 = dkv[:, 256:288], dkv[:, 288:320]
    w["mla_dqkv"] = f(np.concatenate([dq, dkv[:, :256], x1, x2, x2, x1], axis=1))
    uq = np.asarray(inp["mla_w_uq"][0], np.float32).reshape(QL, HEADS, 192)
    q1, q2 = uq[:, :, 128:160], uq[:, :, 160:192]
    w["mla_uq"] = f(np.concatenate([uq[:, :, :128], q1, q2, q2, q1], axis=2).reshape(QL, HEADS * 256))
    w["mla_w_uk"] = f(np.asarray(inp["mla_w_uk"][0], np.float32).reshape(KVL, HEADS * 128))
    w["mla_w_uv"] = f(np.asarray(inp["mla_w_uv"][0], np.float32).reshape(KVL, HEADS * 128))
    w["mla_w_o"] = f(inp["mla_w_o"][0])
    w["cf_w_pw1"] = f(inp["cf_w_pw1"][0])
    w["cf_w_pw2"] = f(inp["cf_w_pw2"][0])
    w["ff_w1"] = f(inp["ff_w1"])
    w["ff_w2"] = f(inp["ff_w2"])
    w["cpk"] = _pack_consts(inp)
    w["rope"] = _rope_table()
    p = np.arange(128)
    w["ident2"] = (p[:, None] % 64 == p[None, :] % 64).astype(np.float32)
    return w


LAUNCH_GROUPS = [[0, 1, 2, 3]]


def run_layers(xT_list, w, layers, trace=False):
    nc = build_program(layers)
    shp = launch_w_shapes(layers)
    sc_idx = [l // 3 for l in layers if l % 3 == 0]
    base = {"cpk": w["cpk"], "rope": w["rope"], "ident2": w["ident2"]}
    for k in shp:
        if k.startswith("sc_"):
            base[k] = np.ascontiguousarray(w[k][sc_idx])
        elif k.startswith("ff_"):
            base[k] = np.ascontiguousarray(w[k][list(layers)])
        else:
            base[k] = w[k]
    in_maps = []
    for b in range(N_CORES):
        m = dict(base)
        m["xT"] = xT_list[b]
        in_maps.append(m)
    res = run_bass_kernel_spmd(nc, in_maps, core_ids=list(range(N_CORES)), trace=trace)
    return [np.asarray(r["outT"]) for r in res.results], res


def kernel(**inputs):
    x = np.asarray(inputs["x"], np.float32)
    w = _host_weights(inputs)
    xT = [np.ascontiguousarray(x[b].T) for b in range(N_CORES)]
    for grp in LAUNCH_GROUPS:
        xT, _ = run_layers(xT, w, grp)
    out = np.stack([np.ascontiguousarray(o.T) for o in xT], axis=0)
    return out.astype(np.float32)
```
